# Optimizing a Trainium2 kernel written in Bass

```python
import jax, jax.numpy as jnp
from jax import lax
import numpy as np

D_MODEL = 4096
BATCH = 2
SEQ = 8192
DEPTH = 4

CHUNK = 64
N_MEM = 256
D_MIX = D_MODEL
D_GROUP = D_MIX // 4
POOL_WINDOWS = (2, 4, 8, 16)
POOL_CH = D_GROUP // 4
SGU_BLOCK = 128
SGU_HEADS = 8
SGU_HEAD_DIM = D_GROUP // SGU_HEADS
CONV_WIDTH = 31
CONV_GROUPS = 8
SC_WIDTH = 3
XA_HEADS = 4
XA_HEAD_DIM = 128
D_XA = XA_HEADS * XA_HEAD_DIM
D_FF = 4 * D_MODEL
D_IN = 8 * D_GROUP
EPS = 1e-6

kernel_name = "hybrid_pool_sgu_conformer_shortconv_encoder"


def rmsnorm(x, g):
    xf = x.astype(jnp.float32)
    y = xf * lax.rsqrt(jnp.mean(xf * xf, axis=-1, keepdims=True) + EPS)
    return (y * g.astype(jnp.float32)).astype(x.dtype)


def layernorm(x, g, b):
    xf = x.astype(jnp.float32)
    mu = jnp.mean(xf, axis=-1, keepdims=True)
    var = jnp.mean(jnp.square(xf - mu), axis=-1, keepdims=True)
    y = (xf - mu) * lax.rsqrt(var + EPS)
    return (y * g.astype(jnp.float32) + b.astype(jnp.float32)).astype(x.dtype)


def group_layernorm(x, g, b, groups):
    B, S, C = x.shape
    xf = x.astype(jnp.float32).reshape(B, S, groups, C // groups)
    mu = jnp.mean(xf, axis=-1, keepdims=True)
    var = jnp.mean(jnp.square(xf - mu), axis=-1, keepdims=True)
    y = ((xf - mu) * lax.rsqrt(var + EPS)).reshape(B, S, C)
    return (y * g.astype(jnp.float32) + b.astype(jnp.float32)).astype(x.dtype)


def causal_dwconv(x, w):
    K, C = w.shape
    return lax.conv_general_dilated(
        x, w[:, None, :].astype(x.dtype), window_strides=(1,), padding=[(K - 1, 0)],
        dimension_numbers=("NWC", "WIO", "NWC"), feature_group_count=C)


def multiscale_pool(a, w_pool, scale):
    B, S, _ = a.shape
    af = a.astype(jnp.float32)
    csum = jnp.pad(jnp.cumsum(af, axis=1), ((0, 0), (1, 0), (0, 0)))
    t1 = jnp.arange(1, S + 1, dtype=jnp.float32)
    outs = []
    for g, w in enumerate(POOL_WINDOWS):
        sl = slice(g * POOL_CH, (g + 1) * POOL_CH)
        cg = csum[:, :, sl]
        upper = cg[:, 1:]
        lower = jnp.pad(cg[:, :S + 1 - w], ((0, 0), (w - 1, 0), (0, 0)))
        mean = (upper - lower) / jnp.minimum(t1, w)[None, :, None]
        d = (mean - af[:, :, sl]).astype(a.dtype)
        outs.append(jnp.einsum("bsc,cd->bsd", d, w_pool[g]))
    return jnp.concatenate(outs, axis=-1) * scale


def spatial_gating(u, v, ln_g, ln_b, w_s, b_s):
    B, S, C = v.shape
    v = layernorm(v, ln_g, ln_b)
    vb = v.reshape(B, S // SGU_BLOCK, SGU_BLOCK, SGU_HEADS, SGU_HEAD_DIM)
    chunk_id = jnp.arange(SGU_BLOCK) // CHUNK
    mask = chunk_id[:, None] >= chunk_id[None, :]
    w = jnp.where(mask[None], w_s, 0).astype(v.dtype)
    f = jnp.einsum("hij,bnjhc->bnihc", w, vb) + b_s.T[None, None, :, :, None]
    return u * f.reshape(B, S, C)


def hybrid_mixer(h, w_in, pool_w, pool_scale, sgu_ln_g, sgu_ln_b, sgu_w, sgu_b,
                 conv_w, conv_b, conv_gn_g, conv_gn_b, sc_w, w_out):
    z = h @ w_in
    a_pool, u, v, c_a, c_gate, sc_bg, sc_cg, sc_h = jnp.split(z, 8, axis=-1)
    y_a = multiscale_pool(a_pool, pool_w, pool_scale)
    y_b = spatial_gating(jax.nn.gelu(u), jax.nn.gelu(v), sgu_ln_g, sgu_ln_b, sgu_w, sgu_b)
    c = c_a * jax.nn.sigmoid(c_gate)
    c = causal_dwconv(c, conv_w) + conv_b
    y_c = jax.nn.silu(group_layernorm(c, conv_gn_g, conv_gn_b, CONV_GROUPS))
    y_d = sc_bg * causal_dwconv(sc_cg * sc_h, sc_w)
    return jnp.concatenate([y_a, y_b, y_c, y_d], axis=-1) @ w_out


def cross_attention(h, m, wq, wk, wv, wo):
    B, S, _ = h.shape
    M = m.shape[1]
    q = (h @ wq).reshape(B, S, XA_HEADS, XA_HEAD_DIM)
    k = (m @ wk).reshape(B, M, XA_HEADS, XA_HEAD_DIM)
    v = (m @ wv).reshape(B, M, XA_HEADS, XA_HEAD_DIM)
    s = jnp.einsum("bshd,bmhd->bhsm", q, k).astype(jnp.float32) * (XA_HEAD_DIM ** -0.5)
    p = jax.nn.softmax(s, axis=-1).astype(v.dtype)
    o = jnp.einsum("bhsm,bmhd->bshd", p, v).reshape(B, S, D_XA)
    return o @ wo


def squared_relu_mlp(h, w1, w2):
    return jnp.square(jax.nn.relu(h @ w1)) @ w2


def setup_inputs(seed: int = 0) -> dict:
    key = jax.random.key(seed)
    ks = jax.random.split(key, 32)
    f32 = jnp.float32
    L, D = DEPTH, D_MODEL

    def nrm(k, shape, scale):
        return jax.random.normal(k, shape, f32) * scale

    return {
        "x": nrm(ks[0], (BATCH, SEQ, D), 1.0),
        "mem": nrm(ks[1], (BATCH, N_MEM, D), 1.0),
        "norm_mix_g": 1.0 + nrm(ks[2], (L, D), 0.02),
        "w_in": nrm(ks[3], (L, D, D_IN), D ** -0.5),
        "pool_w": nrm(ks[4], (L, len(POOL_WINDOWS), POOL_CH, POOL_CH), POOL_CH ** -0.5),
        "pool_scale": 1.0 + nrm(ks[5], (L, D_GROUP), 0.02),
        "sgu_ln_g": 1.0 + nrm(ks[6], (L, D_GROUP), 0.02),
        "sgu_ln_b": nrm(ks[7], (L, D_GROUP), 0.02),
        "sgu_w": nrm(ks[8], (L, SGU_HEADS, SGU_BLOCK, SGU_BLOCK), SGU_BLOCK ** -0.5),
        "sgu_b": 1.0 + nrm(ks[9], (L, SGU_HEADS, SGU_BLOCK), 0.02),
        "conv_w": nrm(ks[10], (L, CONV_WIDTH, D_GROUP), CONV_WIDTH ** -0.5),
        "conv_b": nrm(ks[11], (L, D_GROUP), 0.02),
        "conv_gn_g": 1.0 + nrm(ks[12], (L, D_GROUP), 0.02),
        "conv_gn_b": nrm(ks[13], (L, D_GROUP), 0.02),
        "sc_w": nrm(ks[14], (L, SC_WIDTH, D_GROUP), SC_WIDTH ** -0.5),
        "w_out": nrm(ks[15], (L, D_MIX, D), D_MIX ** -0.5),
        "norm_xa_g": 1.0 + nrm(ks[16], (L, D), 0.02),
        "norm_mem_g": 1.0 + nrm(ks[17], (L, D), 0.02),
        "xa_wq": nrm(ks[18], (L, D, D_XA), D ** -0.5),
        "xa_wk": nrm(ks[19], (L, D, D_XA), D ** -0.5),
        "xa_wv": nrm(ks[20], (L, D, D_XA), D ** -0.5),
        "xa_wo": nrm(ks[21], (L, D_XA, D), D_XA ** -0.5),
        "norm_ffn_g": 1.0 + nrm(ks[22], (L, D), 0.02),
        "ffn_w1": nrm(ks[23], (L, D, D_FF), D ** -0.5),
        "ffn_w2": nrm(ks[24], (L, D_FF, D), D_FF ** -0.5),
        "final_g": 1.0 + nrm(ks[25], (D,), 0.02),
    }


def reference(x, mem, norm_mix_g, w_in, pool_w, pool_scale, sgu_ln_g, sgu_ln_b, sgu_w, sgu_b,
              conv_w, conv_b, conv_gn_g, conv_gn_b, sc_w, w_out, norm_xa_g, norm_mem_g,
              xa_wq, xa_wk, xa_wv, xa_wo, norm_ffn_g, ffn_w1, ffn_w2, final_g):
    for l in range(DEPTH):
        h = rmsnorm(x, norm_mix_g[l])
        x = x + hybrid_mixer(h, w_in[l], pool_w[l], pool_scale[l], sgu_ln_g[l], sgu_ln_b[l],
                             sgu_w[l], sgu_b[l], conv_w[l], conv_b[l], conv_gn_g[l], conv_gn_b[l],
                             sc_w[l], w_out[l])
        x = x + cross_attention(rmsnorm(x, norm_xa_g[l]), rmsnorm(mem, norm_mem_g[l]),
                                xa_wq[l], xa_wk[l], xa_wv[l], xa_wo[l])
        x = x + squared_relu_mlp(rmsnorm(x, norm_ffn_g[l]), ffn_w1[l], ffn_w2[l])
    return rmsnorm(x, final_g)
```

```python
import os as _os
import numpy as np
import concourse.bass as bass
import concourse.mybir as mybir
from concourse.bass_utils import run_bass_kernel_spmd

F32 = mybir.dt.float32
BF16 = mybir.dt.bfloat16
ALU = mybir.AluOpType
AF = mybir.ActivationFunctionType
AX = mybir.AxisListType

D = 4096
NCH = 32
T = 256
HALO = 256
EPS = 1e-6
NMEM = 256
NCORES = 8
GROUP_TILES = 2
NOCC = bool(_os.environ.get('KDEBUG_NOCC'))
GPT = int(_os.environ.get("KDEBUG_GPT", "1"))


STRICT = {"st4", ("tmp", 4), ("tmp", 5)}


class Sched:
    def __init__(self):
        self.q = {e: [] for e in ("pe", "act", "dve", "pool", "sp")}
        self.cnt = {e: 0 for e in ("pe", "act", "dve", "pool")}
        self.ndsem = 8
        self.dcnt = [0] * self.ndsem
        self.dnext = 0
        self.cccnt = 0
        self.waited = {}
        self.bufs = {}

    def _need(self, eng, tok, deps, strict=False):
        if tok is None:
            return
        key, val, owner = tok
        if owner == eng and key == eng and not strict:
            return
        if deps.get(key, 0) < val:
            deps[key] = val

    def op(self, eng, fn, reads=(), writes=(), kind="cmp"):
        deps = {}
        for b in reads:
            st = self.bufs.get(b)
            if st is not None:
                self._need(eng, st[0], deps, strict=(b in STRICT))
        for b in writes:
            st = self.bufs.get(b)
            if st is not None:
                self._need(eng, st[0], deps)
                for tok in st[1].values():
                    self._need(eng, tok, deps)
        if kind == "cmp":
            self.cnt[eng] += 1
            tok = (eng, self.cnt[eng], eng)
            inc = 1
        elif kind == "dma":
            i = self.dnext
            self.dnext = (self.dnext + 1) % self.ndsem
            key = ("d", i)
            if self.dcnt[i] > 0:
                if deps.get(key, 0) < self.dcnt[i]:
                    deps[key] = self.dcnt[i]
            self.dcnt[i] += 16
            tok = (key, self.dcnt[i], "dma")
            inc = 16
        else:
            self.cccnt += 1
            tok = ("cc", self.cccnt, "cc")
            inc = 1
        waits = []
        for key, val in deps.items():
            if self.waited.get((eng, key), 0) < val:
                self.waited[(eng, key)] = val
                waits.append((key, val))
        self.q[eng].append((waits, fn, tok[0], inc))
        for b in reads:
            st = self.bufs.setdefault(b, [None, {}])
            st[1][tok[0]] = tok
        for b in writes:
            self.bufs[b] = [tok, {}]


def weight_tiles(L):
    tiles = []
    for l in range(L):
        lt = []
        seg = [(0, 4), (4, 12), (12, 20), (20, 32)]
        for kb, (a, b) in enumerate(seg):
            for j in range(a, b):
                lt.append(("w_in", 0, j, "A"))
            for j in range(4):
                lt.append(("w_out", kb, j, "B"))
        for nm in ("wq", "wk", "wv"):
            for j in range(2):
                lt.append((nm, 0, j, "A"))
        for j in range(4):
            lt.append(("wo", 0, j, "B"))
        for hg in range(16):
            for j in range(4):
                lt.append(("w1", 0, hg * 4 + j, "A"))
            for j in range(4):
                lt.append(("w2", hg, j, "B"))
        tiles.append(lt)
    return tiles


FOLD = {"w_in": 0, "wq": 1, "wk": 2, "wv": 2, "w1": 3}
WSHAPE = {"w_in": (512, 8192), "w_out": (512, 4096), "wq": (512, 512), "wk": (512, 512), "wv": (512, 512),
          "wo": (128, 4096), "w1": (512, 16384), "w2": (2048, 4096)}


def build(L, NCT):
    NT = (HALO + NCT) // T
    assert (HALO + NCT) % T == 0
    nc = bass.Bass("TRN2", target_bir_lowering=False)
    S = Sched()
    tiles = weight_tiles(L)
    TPL = len(tiles[0])
    groups = []
    for l in range(L):
        s = 0
        while s < TPL:
            n = min(GROUP_TILES, TPL - s)
            groups.append((l, s, n))
            s += n
    gidx = {}
    for g, (l, s, n) in enumerate(groups):
        for i in range(n):
            gidx[(l, s + i)] = (g, i)

    def din(name, shape):
        return nc.dram_tensor(name, list(shape), F32, kind="ExternalInput").ap()

    x_d = din("x", [HALO + NCT, D])
    mem_d = din("mem", [NMEM, D])
    ident_d = din("ident", [128, 128])
    tmask_d = din("tmask", [128, T])
    pinv_d = din("pinv", [128, 64])
    cw_d = din("cw", [128, L * 8 * 31])
    cvec_d = din("cvec", [128, L * 4 * 8])
    scw_d = din("scw", [128, L * 8 * 3])
    fg_d = din("fg", [128, 32])
    gsh_d = din("gsh", [128, L * 4 * 4])
    lnb_d = din("lnb", [128, L * 2048])
    sgub_d = din("sgub", [128, L * 1024])
    sgw_d = din("sgw", [128, L * 1024])
    poolw_d = din("poolw", [128, L * 2048])
    wd = {nm: din(nm + "_s", [L * WSHAPE[nm][0], WSHAPE[nm][1]]) for nm in WSHAPE}
    out_d = nc.dram_tensor("out", [NCT, D], F32, kind="ExternalOutput").ap()
    DUMP = _os.environ.get("KDEBUG_DUMP")
    if DUMP:
        dbg_d = nc.dram_tensor("dbg", [NT * 128, 8 * T], BF16, kind="ExternalOutput").ap()
    nbig = (len(groups) + GPT - 1) // GPT
    Sbig = [nc.dram_tensor(f"S{i}", [GPT * GROUP_TILES * 128, 1024], BF16, kind="Internal").ap() for i in range(nbig)]
    Fbig = [nc.dram_tensor(f"F{i}", [GPT * 8 * GROUP_TILES * 128, 1024], BF16, kind="Internal").ap() for i in range(nbig)]
    Sg = []
    Fg = []
    for g, (l, s, n) in enumerate(groups):
        o = (g % GPT) * GROUP_TILES * 128
        Sg.append(Sbig[g // GPT][o:o + n * 128, :])
        Fg.append(Fbig[g // GPT][8 * o:8 * o + 8 * n * 128, :])

    import contextlib
    es = contextlib.ExitStack()
    with es:
        def sb(name, shape, dt):
            return es.enter_context(nc.sbuf_tensor("sb_" + name, list(shape), dt))

        xT = sb("xT", [128, NCH, T], F32)
        xb = sb("xb", [128, NCH, T], BF16)
        yh = sb("yh", [128, 2, 8, T], BF16)
        wsl = sb("wsl", [128, 2, 8192], BF16)
        zs = sb("zs", [128, 16, T + 32], F32)
        ub = sb("ub", [128, 8, T], BF16)
        dp = sb("dp", [128, 8, T], BF16)
        memT = sb("memT", [128, NCH, NMEM], BF16)
        carry = sb("carry", [128, L, 8, 48], F32)
        cw = sb("cw", [128, L, 8, 31], F32)
        cvec = sb("cvec", [128, L, 4, 8], F32)
        scw = sb("scw", [128, L, 8, 3], F32)
        fg = sb("fg", [128, 32], F32)
        gsh = sb("gsh", [128, L, 4, 4], F32)
        lnb = sb("lnb", [128, 2048], F32)
        sgub = sb("sgub", [128, 1024], F32)
        sgw = sb("sgw", [128, 1024], BF16)
        plw = sb("plw", [128, 2048], BF16)
        vg = sb("vg", [128, 2, 1024], F32)
        vln = sb("vln", [128, 1024], BF16)
        vsq = sb("vsq", [128, 1024], F32)
        sgwf = vsq
        plwf = vg
        st4 = sb("st4", [128, 8], F32)
        KT = sb("KT", [128, 4, NMEM], BF16)
        Vt = sb("Vt", [128, 2, 512], BF16)
        qT = sb("qT", [128, 4, T], BF16)
        ex = sb("ex", [128, 2, T], BF16)
        rden = sb("rden", [128, T], F32)
        R = sb("R", [128, T], F32)
        onesb = sb("onesb", [128, 128], BF16)
        onesf = sb("onesf", [128, 128], F32)
        ident = sb("ident", [128, 128], F32)
        tmask = sb("tmask", [128, T], F32)
        pinv = sb("pinv", [128, 4, 16], F32)
        tmp = sb("tmp", [128, 6, T + 16], F32)
        ps = [es.enter_context(nc.psum_tensor(f"ps{i}", [128, 512], F32)) for i in range(8)]
        sem = {e: es.enter_context(nc.semaphore("s_" + e)) for e in ("pe", "act", "dve", "pool")}
        for i in range(8):
            sem[("d", i)] = es.enter_context(nc.semaphore(f"s_d{i}"))
        sem["cc"] = es.enter_context(nc.semaphore("s_cc"))
        block = es.enter_context(nc.Block())

        psn = [0]
        ZS_ALL = [("zs", c) for c in range(16)]
        TM = lambda *rows: [("tmp", r_) for r_ in rows]

        def bank():
            i = psn[0] % 8
            psn[0] += 1
            return i

        def dma(out, in_, reads, writes):
            S.op("sp", lambda e: e.dma_start(out=out, in_=in_), reads, writes, kind="dma")

        def mmg(bk, out, pairs, reads, extra_w=()):
            def fn(e):
                n = len(pairs)
                for i, (l_, r_) in enumerate(pairs):
                    ins = e.matmul(out, lhsT=l_, rhs=r_, start=(i == 0), stop=(i == n - 1))
                return ins
            S.op("pe", fn, reads, [("ps", bk)] + list(extra_w))

        def tt(eng, out, in0, in1, op, reads, writes):
            S.op(eng, lambda e: e.tensor_tensor(out=out, in0=in0, in1=in1, op=op), reads, writes)

        def ts(eng, out, in0, s1, s2, op0, op1, reads, writes):
            if op1 is None:
                S.op(eng, lambda e: e.tensor_scalar(out=out, in0=in0, scalar1=s1, scalar2=None, op0=op0), reads, writes)
            else:
                S.op(eng, lambda e: e.tensor_scalar(out=out, in0=in0, scalar1=s1, scalar2=s2, op0=op0, op1=op1), reads, writes)

        def stt(eng, out, in0, sc, in1, op0, op1, reads, writes):
            S.op(eng, lambda e: e.scalar_tensor_tensor(out=out, in0=in0, scalar=sc, in1=in1, op0=op0, op1=op1), reads, writes)

        def act(out, in_, func, reads, writes, scale=1.0, bias=0.0):
            S.op("act", lambda e: e.activation(out=out, in_=in_, func=func, bias=bias, scale=scale), reads, writes)

        def cp(eng, out, in_, reads, writes):
            if eng == "act":
                S.op("act", lambda e: e.copy(out=out, in_=in_), reads, writes)
            else:
                S.op(eng, lambda e: e.tensor_copy(out=out, in_=in_), reads, writes)

        def mset(eng, ap, val, writes):
            S.op(eng, lambda e: e.memset(ap, val), (), writes)

        for (t_, d_, nm) in ((cw, cw_d, "cw"), (cvec, cvec_d, "cvec"), (scw, scw_d, "scw"), (fg, fg_d, "fg"),
                             (gsh, gsh_d, "gsh"), (ident, ident_d, "ident"), (tmask, tmask_d, "tmask"),
                             (pinv, pinv_d, "pinv")):
            flat = t_[:] if len(t_.shape) == 2 else t_[:].rearrange(
                "p a b c -> p (a b c)" if len(t_.shape) == 4 else "p a b -> p (a b)")
            dma(flat, d_[:, :], [], [nm])
        mset("dve", onesb[:], 1.0, ["onesb"])
        mset("dve", onesf[:], 1.0 / 128.0, ["onesf"])
        mset("dve", carry[:], 0.0, ["carry"])

        stg_f = [xT[:, 0:16, :].rearrange("p a b -> p (a b)"), xT[:, 16:32, :].rearrange("p a b -> p (a b)")]
        stg_b = [xb[:, 0:16, :].rearrange("p a b -> p (a b)"), xb[:, 16:32, :].rearrange("p a b -> p (a b)")]
        pn = [0]
        SF = lambda sl: [("xT", k) for k in range(16 * sl, 16 * sl + 16)]
        SB = lambda sl: [("xb", k) for k in range(16 * sl, 16 * sl + 16)]
        for g, (l, s, n) in enumerate(groups):
            i = 0
            while i < n:
                nb = min(4, n - i)
                nm, kb, j, kind = tiles[l][s + i]
                nb2 = 1
                while nb2 < nb and tiles[l][s + i + nb2][:2] == (nm, kb) and tiles[l][s + i + nb2][2] == j + nb2:
                    nb2 += 1
                nb = nb2
                sl = pn[0] % 2
                pn[0] += 1
                rows = WSHAPE[nm][0]
                if kind == "A":
                    src = wd[nm][l * rows: l * rows + 512, j * 256:(j + nb) * 256].rearrange("(k p) n -> p k n", p=128)
                    dstv = stg_f[sl][:, 0:nb * 1024].rearrange("p (k n) -> p k n", k=4)
                    dma(dstv, src, [], SF(sl))
                    for t_i in range(nb):
                        for k in range(4):
                            o_ = stg_b[sl][:, t_i * 1024 + k * 256: t_i * 1024 + (k + 1) * 256]
                            i_ = stg_f[sl][:, k * nb * 256 + t_i * 256: k * nb * 256 + (t_i + 1) * 256]
                            eng = "dve" if (k % 2 == 0) else "act"
                            if nm in FOLD:
                                sc = gsh[:, l, FOLD[nm], k:k + 1]
                                if eng == "dve":
                                    ts("dve", o_, i_, sc, None, ALU.mult, None, SF(sl) + ["gsh"], SB(sl))
                                else:
                                    act(o_, i_, AF.Copy, SF(sl) + ["gsh"], SB(sl), scale=sc)
                            else:
                                cp(eng, o_, i_, SF(sl), SB(sl))
                else:
                    r0 = l * rows + kb * 128
                    src = wd[nm][r0:r0 + 128, j * 1024:(j + nb) * 1024]
                    dma(stg_f[sl][:, 0:nb * 1024], src, [], SF(sl))
                    for t_i in range(nb):
                        eng = "dve" if (t_i % 2 == 0) else "act"
                        cp(eng, stg_b[sl][:, t_i * 1024:(t_i + 1) * 1024], stg_f[sl][:, t_i * 1024:(t_i + 1) * 1024],
                           SF(sl), SB(sl))
                dst = Sg[g][i * 128:(i + nb) * 128, :].rearrange("(t p) n -> p t n", p=128)
                dma(dst, stg_b[sl][:, 0:nb * 1024].rearrange("p (t n) -> p t n", t=nb), SB(sl), [("S", g, i)])
                i += nb
            sids = [("S", g, i2) for i2 in range(n)]
            if NOCC:
                continue
            S.op("pool", (lambda e, g=g: e.collective_compute("AllGather", ALU.bypass, replica_groups=[list(range(NCORES))],
                                                                ins=[Sg[g]], outs=[Fg[g]])),
                 [b for b in sids if b in S.bufs], [("F", g)], kind="cc")

        wn = [0]

        def wtile(l, ti):
            g, i = gidx[(l, ti)]
            n = groups[g][2]
            sl = wn[0] % 2
            wn[0] += 1
            src = Fg[g].rearrange("(r t p) n -> t p r n", r=8, p=128)[i]
            dma(wsl[:, sl, :].rearrange("p (r n) -> p r n", r=8), src, [("F", g)], [("w", sl)])
            return (wsl[:, sl, :].rearrange("p (k n) -> p k n", k=32), wsl[:, sl, :].rearrange("p (k n) -> p k n", k=8),
                    ("w", sl))

        def load_tokmajor_T(src_rows, dst, dcol0, dst_id_fn):
            xin = zs[:, :, :].rearrange("p a b -> p (a b)")[:, 0:D]
            dma(xin, src_rows, [], ZS_ALL)
            for k0 in range(0, NCH, 4):
                bk = bank()

                def fn(e, k0=k0, bk=bk):
                    for q in range(4):
                        ins = e.transpose(ps[bk][:, q * 128:(q + 1) * 128], xin[:, (k0 + q) * 128:(k0 + q + 1) * 128], ident[:])
                    return ins
                S.op("pe", fn, ZS_ALL + ["ident"], [("ps", bk)])
                eng = "act" if (k0 // 4) % 2 == 0 else "dve"
                cp(eng, dst[:, k0:k0 + 4, dcol0:dcol0 + 128], ps[bk][:, :].rearrange("p (q n) -> p q n", q=4),
                   [("ps", bk)], [dst_id_fn(k) for k in range(k0, k0 + 4)])

        def norm(src, src_id, dst, dst_id, ncol):
            bk = bank()
            for g4 in range(4):
                sq = (ub if g4 % 2 == 0 else dp)
                sqid = "ub" if g4 % 2 == 0 else "dp"
                act(sq[:, :, 0:ncol], src[:, g4 * 8:(g4 + 1) * 8, 0:ncol], AF.Square,
                    [src_id(k) for k in range(g4 * 8, g4 * 8 + 8)], [sqid])

                def fn(e, g4=g4, sq=sq, bk=bk):
                    for k in range(8):
                        ins = e.matmul(ps[bk][:, 0:ncol], lhsT=onesb[:], rhs=sq[:, k, 0:ncol],
                                       start=(g4 == 0 and k == 0), stop=(g4 == 3 and k == 7))
                    return ins
                S.op("pe", fn, [sqid, "onesb"], [("ps", bk)])
            ts("dve", R[:, 0:ncol], ps[bk][:, 0:ncol], 1.0 / D, EPS, ALU.mult, ALU.add, [("ps", bk)], ["R"])
            act(R[:, 0:ncol], R[:, 0:ncol], AF.Sqrt, ["R"], ["R"])
            S.op("dve", lambda e: e.reciprocal(out=R[:, 0:ncol], in_=R[:, 0:ncol]), ["R"], ["R"])
            for k in range(NCH):
                tt("dve", dst[:, k, 0:ncol], src[:, k, 0:ncol], R[:, 0:ncol], ALU.mult, [src_id(k), "R"], [dst_id(k)])

        xid = lambda k: ("xT", k)
        xbid = lambda k: ("xb", k)

        def lin_fm(wv, kcs, NC, rhs_fn, rhs_ids, wid, evac, ncol=T):
            for m in range(NC // 128):
                bk = bank()
                mmg(bk, ps[bk][:, 0:ncol], [(wv[:, k, m * 128:(m + 1) * 128], rhs_fn(k)) for k in kcs],
                    [wid] + list(rhs_ids))
                evac(m, bk)

        def accum_x(l, names, nk, src_fn, src_ids):
            for j in range(4):
                wA, wB, wid = wtile(l, names[j])

                def ev(m, bk, j=j):
                    n_ = j * 8 + m
                    tt("dve", xT[:, n_, :], xT[:, n_, :], ps[bk][:, 0:T], ALU.add, [("ps", bk), xid(n_)], [xid(n_)])
                lin_fm(wB, range(nk), 1024, src_fn, src_ids, wid, ev)

        def gelu_to(out_ap, bk, ncol, ra, rb, out_ids):
            p_ = ps[bk][:, 0:ncol]
            tA = tmp[:, ra, 0:ncol]
            tB = tmp[:, rb, 0:ncol]
            ia = TM(ra)
            ib = TM(rb)
            act(tA, p_, AF.Square, [("ps", bk)], ia)
            ts("dve", tA, tA, 0.044715, 1.0, ALU.mult, ALU.add, ia, ia)
            tt("dve", tB, tA, p_, ALU.mult, ia + [("ps", bk)], ib)
            act(tB, tB, AF.Sigmoid, ib, ib, scale=1.5957691216057308)
            tt("dve", out_ap, p_, tB, ALU.mult, ib + [("ps", bk)], out_ids)

        out_sem_ids = []
        for b in range(NMEM // 128):
            load_tokmajor_T(mem_d[b * 128:(b + 1) * 128, :], xT, b * 128, xid)
        norm(xT, xid, memT, lambda k: ("memT", k), NMEM)
        memT_ids = [("memT", k) for k in range(NCH)]

        tix = {}
        for ti, tdesc in enumerate(tiles[0]):
            tix.setdefault((tdesc[0], tdesc[1]), []).append(ti)

        for it in range(NT):
            first = (it == 0)
            for b in range(T // 128):
                r0 = it * T + b * 128
                load_tokmajor_T(x_d[r0:r0 + 128, :], xT, b * 128, xid)
            for l in range(L):
                norm(xT, xid, xb, xbid, T)
                xb_ids = [xbid(k) for k in range(NCH)]
                dma(lnb[:], lnb_d[:, l * 2048:(l + 1) * 2048], [], ["lnb"])
                dma(sgub[:], sgub_d[:, l * 1024:(l + 1) * 1024], [], ["sgub"])
                dma(sgwf[:], sgw_d[:, l * 1024:(l + 1) * 1024], [], ["vsq"])
                dma(plwf[:].rearrange("p a b -> p (a b)"), poolw_d[:, l * 2048:(l + 1) * 2048], [], [("vg", 0), ("vg", 1)])
                cp("pool", sgw[:], sgwf[:], ["vsq"], ["sgw"])
                sgw3 = sgw[:].rearrange("p (h i) -> p h i", h=8)
                mset("pool", sgw3[64:128, :, 0:64], 0.0, ["sgw"])
                cp("pool", plw[:], plwf[:].rearrange("p a b -> p (a b)"), [("vg", 0), ("vg", 1)], ["plw"])
                win = tix[("w_in", 0)]
                yb = 0
                cp("dve", zs[:, 0:8, 17:32], carry[:, l, :, 30:45], ["carry"], [("zs", c) for c in range(8)])
                for j in range(4):
                    wA, wB, wid = wtile(l, win[j])

                    def ev(m, bk, j=j):
                        c = j * 2 + m
                        cp("act", zs[:, c, 32:32 + T], ps[bk][:, 0:T], [("ps", bk)], [("zs", c)])
                        if first:
                            tt("dve", zs[:, c, 32:32 + T], zs[:, c, 32:32 + T], tmask[:], ALU.mult, [("zs", c), "tmask"], [("zs", c)])
                    lin_fm(wA, range(32), 256, lambda k: xb[:, k, :], xb_ids, wid, ev)
                cp("dve", carry[:, l, :, 30:45], zs[:, 0:8, 32 + T - 15:32 + T], [("zs", c) for c in range(8)], ["carry"])
                for g in range(4):
                    w = 2 << g
                    zid = [("zs", 2 * g), ("zs", 2 * g + 1)]
                    cur = zs[:, 2 * g:2 * g + 2, 17:32 + T]
                    ln = T + 15
                    sh = 1
                    pp = 0
                    while sh < w:
                        nxt = tmp[:, 2 * pp:2 * pp + 2, 0:ln - sh]
                        tt("dve", nxt, cur[:, :, sh:ln], cur[:, :, 0:ln - sh], ALU.add, zid + TM(0, 1, 2, 3), TM(0, 1, 2, 3))
                        cur = nxt
                        ln -= sh
                        sh *= 2
                        pp = 1 - pp
                    off = ln - T
                    stt("dve", dp[:, 2 * g:2 * g + 2, :], cur[:, :, off:off + T], 1.0 / w, zs[:, 2 * g:2 * g + 2, 32:32 + T],
                        ALU.mult, ALU.subtract, zid + TM(0, 1, 2, 3), ["dp"])
                    if it == 1:
                        for cc in range(2):
                            t4 = tmp[:, 4, 0:16]
                            tt("dve", t4, cur[:, cc, off:off + 16], pinv[:, g, :], ALU.mult, TM(0, 1, 2, 3) + ["pinv"], TM(4))
                            tt("dve", dp[:, 2 * g + cc, 0:16], t4, zs[:, 2 * g + cc, 32:48], ALU.subtract,
                               TM(4) + zid, ["dp"])
                plw4 = plw[:].rearrange("p (g c d) -> p g c d", g=4, c=2)
                for g in range(4):
                    for dc in range(2):
                        bk = bank()
                        mmg(bk, ps[bk][:, 0:T], [(plw4[:, g, cc, dc * 128:(dc + 1) * 128], dp[:, 2 * g + cc, :]) for cc in range(2)],
                            ["plw", "dp"])
                        ch = 2 * g + dc
                        act(yh[:, yb, ch, :], ps[bk][:, 0:T], AF.Copy, [("ps", bk), "cvec"], [("yh", yb, ch)],
                            scale=cvec[:, l, 0, ch:ch + 1])
                accum_x(l, tix[("w_out", 0)], 8, lambda k, yb=yb: yh[:, yb, k, :], [("yh", yb, k) for k in range(8)])
                yb = 1 - yb
                for j in range(4):
                    wA, wB, wid = wtile(l, win[4 + j])

                    def ev(m, bk, j=j):
                        c = j * 2 + m
                        gelu_to(ub[:, c, :], bk, T, 0, 1, ["ub"])
                    lin_fm(wA, range(32), 256, lambda k: xb[:, k, :], xb_ids, wid, ev)
                for j in range(4):
                    wA, wB, wid = wtile(l, win[8 + j])
                    for b in range(T // 128):
                        bk = bank()
                        mmg(bk, ps[bk][:, 0:256], [(xb[:, k, b * 128:(b + 1) * 128], wA[:, k, :]) for k in range(32)],
                            [wid] + xb_ids)
                        gelu_to(vg[:, b, j * 256:(j + 1) * 256], bk, 256, 2, 3, [("vg", b)])
                sgw3 = sgw[:].rearrange("p (h i) -> p h i", h=8)
                sgub3 = sgub[:].rearrange("p (h i) -> p h i", h=8)
                for b in range(T // 128):
                    v_ = vg[:, b, :]
                    vid = ("vg", b)
                    S.op("dve", lambda e, v_=v_: e.reduce_sum(out=st4[:, 0:1], in_=v_, axis=AX.X), [vid], ["st4"])
                    ts("dve", st4[:, 1:2], st4[:, 0:1], 1.0 / 1024, None, ALU.mult, None, ["st4"], ["st4"])
                    ts("dve", v_, v_, st4[:, 1:2], None, ALU.subtract, None, [vid, "st4"], [vid])
                    tt("dve", vsq[:], v_, v_, ALU.mult, [vid], ["vsq"])
                    S.op("dve", lambda e: e.reduce_sum(out=st4[:, 2:3], in_=vsq[:], axis=AX.X), ["vsq"], ["st4"])
                    ts("dve", st4[:, 3:4], st4[:, 2:3], 1.0 / 1024, EPS, ALU.mult, ALU.add, ["st4"], ["st4"])
                    act(st4[:, 3:4], st4[:, 3:4], AF.Sqrt, ["st4"], ["st4"])
                    S.op("dve", lambda e: e.reciprocal(out=st4[:, 3:4], in_=st4[:, 3:4]), ["st4"], ["st4"])
                    stt("dve", vsq[:], v_, st4[:, 3:4], lnb[:, 0:1024], ALU.mult, ALU.mult, [vid, "st4", "lnb"], ["vsq"])
                    tt("dve", vln[:], vsq[:], lnb[:, 1024:2048], ALU.add, ["vsq", "lnb"], ["vln"])
                    for h0 in range(0, 8, 4):
                        bk = bank()

                        def fn(e, h0=h0, bk=bk):
                            for q in range(4):
                                h = h0 + q
                                ins = e.matmul(ps[bk][:, q * 128:(q + 1) * 128], lhsT=vln[:, h * 128:(h + 1) * 128],
                                               rhs=sgw3[:, h, :], start=True, stop=True)
                            return ins
                        S.op("pe", fn, ["vln", "sgw"], [("ps", bk)])
                        for q in range(4):
                            h = h0 + q
                            t5 = tmp[:, 5, 0:128]
                            tt("dve", t5, ps[bk][:, q * 128:(q + 1) * 128], sgub3[:, h, :], ALU.add, [("ps", bk), "sgub"], TM(5))
                            tt("dve", yh[:, yb, h, b * 128:(b + 1) * 128], t5, ub[:, h, b * 128:(b + 1) * 128], ALU.mult,
                               TM(5) + ["ub"], [("yh", yb, h)])
                if DUMP == "b" and l == 0:
                    dma(dbg_d[it * 128:(it + 1) * 128, :], yh[:, yb, :, :].rearrange("p a b -> p (a b)"),
                        [("yh", yb, k) for k in range(8)], [("dbg", it)])
                    out_sem_ids.append(("dbg", it))
                accum_x(l, tix[("w_out", 1)], 8, lambda k, yb=yb: yh[:, yb, k, :], [("yh", yb, k) for k in range(8)])
                yb = 1 - yb
                cp("dve", zs[:, 0:8, 2:32], carry[:, l, :, 0:30], ["carry"], [("zs", c) for c in range(8)])
                for j in range(4):
                    wA, wB, wid = wtile(l, win[12 + j])

                    def ev(m, bk, j=j):
                        c = j * 2 + m
                        cp("act", zs[:, c, 32:32 + T], ps[bk][:, 0:T], [("ps", bk)], [("zs", c)])
                    lin_fm(wA, range(32), 256, lambda k: xb[:, k, :], xb_ids, wid, ev)
                for j in range(4):
                    wA, wB, wid = wtile(l, win[16 + j])

                    def ev(m, bk, j=j):
                        c = j * 2 + m
                        t0 = tmp[:, 0, 0:T]
                        act(t0, ps[bk][:, 0:T], AF.Sigmoid, [("ps", bk)], TM(0))
                        tt("dve", zs[:, c, 32:32 + T], zs[:, c, 32:32 + T], t0, ALU.mult, TM(0) + [("zs", c)], [("zs", c)])
                        if first:
                            tt("dve", zs[:, c, 32:32 + T], zs[:, c, 32:32 + T], tmask[:], ALU.mult, [("zs", c), "tmask"], [("zs", c)])
                    lin_fm(wA, range(32), 256, lambda k: xb[:, k, :], xb_ids, wid, ev)
                cp("dve", carry[:, l, :, 0:30], zs[:, 0:8, 32 + T - 30:32 + T], [("zs", c) for c in range(8)], ["carry"])
                for c in range(8):
                    a_ = tmp[:, 2, 0:T]
                    ts("dve", a_, zs[:, c, 2:2 + T], cw[:, l, c, 0:1], None, ALU.mult, None, [("zs", c), "cw"], [("tmp", 2)])
                    for k in range(1, 31):
                        stt("dve", a_, zs[:, c, 2 + k:2 + k + T], cw[:, l, c, k:k + 1], a_, ALU.mult, ALU.add,
                            [("zs", c), "cw", ("tmp", 2)], [("tmp", 2)])
                    ts("dve", a_, a_, cvec[:, l, 1, c:c + 1], None, ALU.add, None, [("tmp", 2), "cvec"], [("tmp", 2)])
                    bk = bank()
                    mmg(bk, ps[bk][:, 0:T], [(onesf[:], a_)], [("tmp", 2), "onesf"])
                    tt("dve", a_, a_, ps[bk][:, 0:T], ALU.subtract, [("tmp", 2), ("ps", bk)], [("tmp", 2)])
                    s_ = tmp[:, 3, 0:T]
                    act(s_, a_, AF.Square, [("tmp", 2)], [("tmp", 3)])
                    bk = bank()
                    mmg(bk, ps[bk][:, 0:T], [(onesf[:], s_)], [("tmp", 3), "onesf"])
                    ts("dve", s_, ps[bk][:, 0:T], EPS, None, ALU.add, None, [("ps", bk)], [("tmp", 3)])
                    act(s_, s_, AF.Sqrt, [("tmp", 3)], [("tmp", 3)])
                    S.op("dve", lambda e, s_=s_: e.reciprocal(out=s_, in_=s_), [("tmp", 3)], [("tmp", 3)])
                    tt("dve", a_, a_, s_, ALU.mult, [("tmp", 2), ("tmp", 3)], [("tmp", 2)])
                    act(yh[:, yb, c, :], a_, AF.Silu, [("tmp", 2), "cvec"], [("yh", yb, c)],
                        scale=cvec[:, l, 2, c:c + 1], bias=cvec[:, l, 3, c:c + 1])
                accum_x(l, tix[("w_out", 2)], 8, lambda k, yb=yb: yh[:, yb, k, :], [("yh", yb, k) for k in range(8)])
                yb = 1 - yb
                cp("dve", zs[:, 0:8, 30:32], carry[:, l, :, 45:47], ["carry"], [("zs", c) for c in range(8)])
                for j in range(4):
                    wA, wB, wid = wtile(l, win[20 + j])

                    def ev(m, bk, j=j):
                        c = j * 2 + m
                        cp("act", zs[:, 8 + c, 32:32 + T], ps[bk][:, 0:T], [("ps", bk)], [("zs", 8 + c)])
                    lin_fm(wA, range(32), 256, lambda k: xb[:, k, :], xb_ids, wid, ev)
                for j in range(4):
                    wA, wB, wid = wtile(l, win[24 + j])

                    def ev(m, bk, j=j):
                        c = j * 2 + m
                        cp("act", zs[:, c, 32:32 + T], ps[bk][:, 0:T], [("ps", bk)], [("zs", c)])
                    lin_fm(wA, range(32), 256, lambda k: xb[:, k, :], xb_ids, wid, ev)
                for j in range(4):
                    wA, wB, wid = wtile(l, win[28 + j])

                    def ev(m, bk, j=j):
                        c = j * 2 + m
                        tt("dve", zs[:, c, 32:32 + T], zs[:, c, 32:32 + T], ps[bk][:, 0:T], ALU.mult, [("ps", bk), ("zs", c)], [("zs", c)])
                        if first:
                            tt("dve", zs[:, c, 32:32 + T], zs[:, c, 32:32 + T], tmask[:], ALU.mult, [("zs", c), "tmask"], [("zs", c)])
                        a_ = tmp[:, 2, 0:T]
                        ts("dve", a_, zs[:, c, 30:30 + T], scw[:, l, c, 0:1], None, ALU.mult, None, [("zs", c), "scw"], [("tmp", 2)])
                        for k in range(1, 3):
                            stt("dve", a_, zs[:, c, 30 + k:30 + k + T], scw[:, l, c, k:k + 1], a_, ALU.mult, ALU.add,
                                [("zs", c), "scw", ("tmp", 2)], [("tmp", 2)])
                        tt("dve", yh[:, yb, c, :], a_, zs[:, 8 + c, 32:32 + T], ALU.mult, [("tmp", 2), ("zs", 8 + c)], [("yh", yb, c)])
                    lin_fm(wA, range(32), 256, lambda k: xb[:, k, :], xb_ids, wid, ev)
                cp("dve", carry[:, l, :, 45:47], zs[:, 0:8, 32 + T - 2:32 + T], [("zs", c) for c in range(8)], ["carry"])
                accum_x(l, tix[("w_out", 3)], 8, lambda k, yb=yb: yh[:, yb, k, :], [("yh", yb, k) for k in range(8)])
                yb = 1 - yb

                norm(xT, xid, xb, xbid, T)
                for j in range(2):
                    wA, wB, wid = wtile(l, tix[("wq", 0)][j])

                    def ev(m, bk, j=j):
                        cp("act", qT[:, j * 2 + m, :], ps[bk][:, 0:T], [("ps", bk)], ["qT"])
                    lin_fm(wA, range(32), 256, lambda k: xb[:, k, :], xb_ids, wid, ev)
                for j in range(2):
                    wA, wB, wid = wtile(l, tix[("wk", 0)][j])

                    def ev(m, bk, j=j):
                        cp("act", KT[:, j * 2 + m, :], ps[bk][:, 0:NMEM], [("ps", bk)], ["KT"])
                    lin_fm(wA, range(32), 256, lambda k: memT[:, k, :], memT_ids, wid, ev, ncol=NMEM)
                for j in range(2):
                    wA, wB, wid = wtile(l, tix[("wv", 0)][j])
                    for mb in range(2):
                        bk = bank()
                        mmg(bk, ps[bk][:, 0:256], [(memT[:, k, mb * 128:(mb + 1) * 128], wA[:, k, :]) for k in range(32)],
                            [wid] + memT_ids)
                        cp("act", Vt[:, mb, j * 256:(j + 1) * 256], ps[bk][:, 0:256], [("ps", bk)], ["Vt"])
                for h in range(4):
                    bk = bank()

                    def fn(e, h=h, bk=bk):
                        for mh in range(2):
                            ins = e.matmul(ps[bk][:, mh * T:(mh + 1) * T], lhsT=KT[:, h, mh * 128:(mh + 1) * 128], rhs=qT[:, h, :],
                                           start=True, stop=True)
                        return ins
                    S.op("pe", fn, ["KT", "qT"], [("ps", bk)])
                    act(ex[:].rearrange("p a b -> p (a b)"), ps[bk][:, 0:2 * T], AF.Exp, [("ps", bk)], ["ex"], scale=128.0 ** -0.5)
                    bk = bank()
                    mmg(bk, ps[bk][:, 0:T], [(onesb[:], ex[:, mh, :]) for mh in range(2)], ["ex", "onesb"])
                    S.op("dve", lambda e, bk=bk: e.reciprocal(out=rden[:], in_=ps[bk][:, 0:T]), [("ps", bk)], ["rden"])
                    bk = bank()
                    mmg(bk, ps[bk][:, 0:T], [(Vt[:, mh, h * 128:(h + 1) * 128], ex[:, mh, :]) for mh in range(2)], ["ex", "Vt"])
                    tt("dve", ub[:, h, :], ps[bk][:, 0:T], rden[:], ALU.mult, [("ps", bk), "rden"], ["ub"])
                accum_x(l, tix[("wo", 0)], 4, lambda k: ub[:, k, :], ["ub"])

                norm(xT, xid, xb, xbid, T)
                for hg in range(16):
                    for j in range(4):
                        wA, wB, wid = wtile(l, tix[("w1", 0)][hg * 4 + j])

                        def ev(m, bk, j=j, yb=yb):
                            c = j * 2 + m
                            t0 = tmp[:, c % 2, 0:T]
                            act(t0, ps[bk][:, 0:T], AF.Relu, [("ps", bk)], TM(c % 2))
                            tt("pool" if c % 2 else "dve", yh[:, yb, c, :], t0, t0, ALU.mult, TM(c % 2), [("yh", yb, c)])
                        lin_fm(wA, range(32), 256, lambda k: xb[:, k, :], xb_ids, wid, ev)
                    accum_x(l, tix[("w2", hg)], 8, lambda k, yb=yb: yh[:, yb, k, :], [("yh", yb, k) for k in range(8)])
                    yb = 1 - yb

            bkR = bank()
            for g4 in range(4):
                sq = (ub if g4 % 2 == 0 else dp)
                sqid = "ub" if g4 % 2 == 0 else "dp"
                act(sq[:, :, :], xT[:, g4 * 8:(g4 + 1) * 8, :], AF.Square, [xid(k) for k in range(g4 * 8, g4 * 8 + 8)], [sqid])

                def fn(e, g4=g4, sq=sq, bk=bkR):
                    for k in range(8):
                        ins = e.matmul(ps[bk][:, 0:T], lhsT=onesb[:], rhs=sq[:, k, :], start=(g4 == 0 and k == 0),
                                       stop=(g4 == 3 and k == 7))
                    return ins
                S.op("pe", fn, [sqid, "onesb"], [("ps", bkR)])
            ts("dve", R[:], ps[bkR][:, 0:T], 1.0 / D, EPS, ALU.mult, ALU.add, [("ps", bkR)], ["R"])
            act(R[:], R[:], AF.Sqrt, ["R"], ["R"])
            S.op("dve", lambda e: e.reciprocal(out=R[:], in_=R[:]), ["R"], ["R"])
            for k in range(NCH):
                stt("dve", xT[:, k, :], xT[:, k, :], fg[:, k:k + 1], R[:], ALU.mult, ALU.mult, [xid(k), "R", "fg"], [xid(k)])
            for b in range(T // 128):
                r0 = it * T + b * 128 - HALO
                if r0 < 0:
                    continue
                xo = zs[:, :, :].rearrange("p a b -> p (a b)")[:, 0:D]
                for k0 in range(0, NCH, 4):
                    bk = bank()

                    def fn(e, k0=k0, bk=bk, b=b):
                        for q in range(4):
                            ins = e.transpose(ps[bk][:, q * 128:(q + 1) * 128], xT[:, k0 + q, b * 128:(b + 1) * 128], ident[:])
                        return ins
                    S.op("pe", fn, [xid(k) for k in range(k0, k0 + 4)] + ["ident"], [("ps", bk)])
                    eng = "act" if (k0 // 4) % 2 == 0 else "dve"
                    cp(eng, xo[:, k0 * 128:(k0 + 4) * 128], ps[bk][:, :], [("ps", bk)], ZS_ALL)
                dma(out_d[r0:r0 + 128, :], xo, ZS_ALL, [("out", it, b)])
                out_sem_ids.append(("out", it, b))
        fin = {}
        for b in out_sem_ids:
            key, val, _ = S.bufs[b][0]
            fin[key] = max(fin.get(key, 0), val)

        engmap = {"pe": "tensor", "act": "scalar", "dve": "vector", "pool": "gpsimd", "sp": "sync"}

        def emit(eng_name, e):
            for waits, fn, skey, inc in S.q[eng_name]:
                for key, val in waits:
                    e.wait_ge(sem[key], val)
                ins = fn(e)
                if skey == "cc":
                    ins.then_inc(sem[skey])
                else:
                    ins.then_inc(sem[skey], inc)
            if eng_name == "sp":
                for key, val in fin.items():
                    e.wait_ge(sem[key], val)

        @block.tensor
        def _(e):
            emit("pe", e)

        @block.scalar
        def _(e):
            emit("act", e)

        @block.vector
        def _(e):
            emit("dve", e)

        @block.gpsimd
        def _(e):
            emit("pool", e)

        @block.sync
        def _(e):
            emit("sp", e)
    return nc


def _prep_inputs(inp):
    f = lambda a: np.ascontiguousarray(np.asarray(a, dtype=np.float32))
    x = f(inp["x"])
    B, SEQ, _ = x.shape
    L = inp["norm_mix_g"].shape[0]
    cpb = NCORES // B
    NCT = SEQ // cpb
    mem = f(inp["mem"])
    ident = np.eye(128, dtype=np.float32)

    def chan(v):
        v = np.asarray(v, np.float32)
        return v.reshape(v.shape[:-1] + (8, 128))

    conv_w = chan(inp["conv_w"])
    cw = np.ascontiguousarray(conv_w.transpose(3, 0, 2, 1)).reshape(128, -1)
    cv = np.stack([chan(inp["pool_scale"]), chan(inp["conv_b"]), chan(inp["conv_gn_g"]), chan(inp["conv_gn_b"])], 1)
    cvec = np.ascontiguousarray(cv.transpose(3, 0, 1, 2)).reshape(128, -1)
    scw = np.ascontiguousarray(chan(inp["sc_w"]).transpose(3, 0, 2, 1)).reshape(128, -1)
    fg = np.ascontiguousarray(f(inp["final_g"]).reshape(32, 128).T)
    lnb = np.concatenate([f(inp["sgu_ln_g"]), f(inp["sgu_ln_b"])], 1)
    lnb = np.ascontiguousarray(np.broadcast_to(lnb.reshape(1, -1), (128, L * 2048)))
    sgub = np.ascontiguousarray(np.broadcast_to(f(inp["sgu_b"]).reshape(1, -1), (128, L * 1024)))
    sgw = np.ascontiguousarray(f(inp["sgu_w"]).transpose(3, 0, 1, 2)).reshape(128, -1)
    pw = f(inp["pool_w"]).reshape(L, 4, 2, 128, 256)
    poolw = np.ascontiguousarray(pw.transpose(3, 0, 1, 2, 4)).reshape(128, -1)
    gains = np.stack([f(inp["norm_mix_g"]), f(inp["norm_xa_g"]), f(inp["norm_mem_g"]), f(inp["norm_ffn_g"])], 1)
    win, wout, wq, wk, wv, wo, w1, w2 = (inp[k] for k in ("w_in", "w_out", "xa_wq", "xa_wk", "xa_wv", "xa_wo", "ffn_w1", "ffn_w2"))
    maps = []
    for c in range(NCORES):
        b = c // cpb
        s0 = (c % cpb) * NCT
        xs = np.zeros((HALO + NCT, D), np.float32)
        if s0 == 0:
            xs[HALO:] = x[b, 0:NCT]
        else:
            xs[:] = x[b, s0 - HALO:s0 + NCT]
        tmask = np.full((128, T), 0.0 if s0 == 0 else 1.0, np.float32)
        pinv = np.zeros((128, 4, 16), np.float32)
        for g in range(4):
            w = 2 << g
            for t in range(16):
                pinv[:, g, t] = np.float32(1.0) / np.float32(min(t + 1, w) if s0 == 0 else w)
        r = c
        gsh = np.ascontiguousarray(gains[:, :, 512 * r:512 * (r + 1)].reshape(L, 4, 4, 128).transpose(3, 0, 1, 2)).reshape(128, -1)
        rowsA = lambda W: np.ascontiguousarray(np.asarray(W, np.float32)[:, 512 * r:512 * (r + 1), :]).reshape(L * 512, -1)

        def rowsB(W):
            W = np.asarray(W, np.float32)
            KB = W.shape[1] // 1024
            return np.ascontiguousarray(W.reshape(L, KB, 8, 128, W.shape[2])[:, :, r]).reshape(L * KB * 128, -1)
        if r < 4:
            wos = np.ascontiguousarray(np.asarray(wo, np.float32)[:, 128 * r:128 * (r + 1), :]).reshape(L * 128, -1)
        else:
            wos = np.zeros((L * 128, D), np.float32)
        maps.append({
            "x": xs, "mem": np.ascontiguousarray(mem[b]), "ident": ident, "tmask": tmask, "pinv": pinv.reshape(128, 64),
            "cw": cw, "cvec": cvec, "scw": scw, "fg": fg, "gsh": gsh, "lnb": lnb, "sgub": sgub, "sgw": sgw, "poolw": poolw,
            "w_in_s": rowsA(win), "w_out_s": rowsB(wout), "wq_s": rowsA(wq), "wk_s": rowsA(wk), "wv_s": rowsA(wv),
            "wo_s": wos, "w1_s": rowsA(w1), "w2_s": rowsB(w2),
        })
    return maps, L, NCT, B, SEQ


def kernel(**inputs):
    maps, L, NCT, B, SEQ = _prep_inputs(inputs)
    nc = build(L, NCT)
    res = run_bass_kernel_spmd(nc, maps, core_ids=list(range(NCORES)))
    outs = [np.asarray(r["out"], dtype=np.float32) for r in res.results]
    if _os.environ.get("KDEBUG_DUMP"):
        kernel.dbg = [np.asarray(r["dbg"]) for r in res.results]
    return np.concatenate(outs, 0).reshape(B, SEQ, D)
```
